# Optimizing a Trainium2 kernel written in Bass

```python
import jax, jax.numpy as jnp
from jax import lax
import numpy as np

D_MODEL = 2048
BATCH = 8
SEQ = 4096
DEPTH = 1

CHUNK = 64
N_MEM = 256
HG_WIDTH = D_MODEL // 2
HG_KDIM = 128
HG_HEADS = HG_WIDTH // HG_KDIM
HG_VDIM = HG_WIDTH // HG_HEADS
LRU_WIDTH = D_MODEL // 2
LRU_BLOCKS = 8
LRU_BLOCK = LRU_WIDTH // LRU_BLOCKS
CONV_WIDTH = 4
LRU_C = 8.0
XA_HEADS = 4
XA_HEAD_DIM = D_MODEL // XA_HEADS
D_FF = 256 * ((8 * D_MODEL // 3 + 255) // 256)
N_BRANCH = 2
IN_COLS = 4 * HG_WIDTH + 2 * LRU_WIDTH + N_BRANCH * D_MODEL
EPS = 1e-6

kernel_name = 'hybrid_hgrn2_rglru_gated_merge_macaron'


def rms_norm(x, g):
    xf = x.astype(jnp.float32)
    y = xf * lax.rsqrt(jnp.mean(xf * xf, axis=-1, keepdims=True) + EPS)
    return (y * g.astype(jnp.float32)).astype(x.dtype)


def swiglu(h, w_up, w_down):
    gate, up = jnp.split(h @ w_up, 2, axis=-1)
    return (jax.nn.silu(gate) * up) @ w_down


def hgrn2_chunked(q, f_logit, v, lb):
    B, S, H, K = q.shape
    V = v.shape[-1]
    nc = S // CHUNK
    qf = jax.nn.silu(q.astype(jnp.float32))
    f = lb + (1.0 - lb) * jax.nn.sigmoid(f_logit.astype(jnp.float32))
    log_f = jnp.log(f)
    k = 1.0 - f
    vf = v.astype(jnp.float32)

    def to_chunks(t):
        return t.reshape(B, nc, CHUNK, H, t.shape[-1]).transpose(1, 0, 3, 2, 4)

    tri = jnp.tril(jnp.ones((CHUNK, CHUNK), dtype=bool))

    def step(state, inp):
        qc, kc, vc, gc = inp
        b = jnp.cumsum(gc, axis=2)
        o_inter = jnp.einsum('bhtk,bhkv->bhtv', qc * jnp.exp(b), state)
        rel = b[:, :, :, None, :] - b[:, :, None, :, :]
        decay = jnp.where(tri[None, None, :, :, None], jnp.exp(jnp.minimum(rel, 0.0)), 0.0)
        scores = jnp.einsum('bhtk,bhsk,bhtsk->bhts', qc, kc, decay)
        o_intra = jnp.einsum('bhts,bhsv->bhtv', scores, vc)
        b_last = b[:, :, -1, :]
        k_dec = kc * jnp.exp(b_last[:, :, None, :] - b)
        state = jnp.exp(b_last)[..., None] * state + jnp.einsum('bhsk,bhsv->bhkv', k_dec, vc)
        return state, o_inter + o_intra

    state0 = jnp.zeros((B, H, K, V), jnp.float32)
    _, o = lax.scan(step, state0, (to_chunks(qf), to_chunks(k), to_chunks(vf), to_chunks(log_f)))
    return o.transpose(1, 0, 3, 2, 4).reshape(B, S, H, V)


def rglru_branch(xb, gate, conv_w, conv_b, wa, ba, wx, bx, lam):
    B, S, W = xb.shape
    xp = jnp.pad(xb, ((0, 0), (CONV_WIDTH - 1, 0), (0, 0)))
    xc = conv_b
    for j in range(CONV_WIDTH):
        xc = xc + xp[:, CONV_WIDTH - 1 - j:CONV_WIDTH - 1 - j + S, :] * conv_w[j]
    blocks = xc.reshape(B, S, LRU_BLOCKS, LRU_BLOCK)
    r = jax.nn.sigmoid((jnp.einsum('bsnh,nhk->bsnk', blocks, wa).reshape(B, S, W) + ba).astype(jnp.float32))
    i = jax.nn.sigmoid((jnp.einsum('bsnh,nhk->bsnk', blocks, wx).reshape(B, S, W) + bx).astype(jnp.float32))
    log_a = LRU_C * r * jax.nn.log_sigmoid(lam.astype(jnp.float32))
    a = jnp.exp(log_a)
    mult = jnp.sqrt(-jnp.expm1(2.0 * log_a))
    mult = jnp.where((jnp.arange(S) == 0)[None, :, None], 1.0, mult)
    u = xc.astype(jnp.float32) * i * mult

    def combine(c1, c2):
        a1, b1 = c1
        a2, b2 = c2
        return (a1 * a2, a2 * b1 + b2)

    _, h = lax.associative_scan(combine, (a, u), axis=1)
    return (h * jax.nn.gelu(gate.astype(jnp.float32))).astype(xb.dtype)


def cross_attention(h, m, wq, wkv, wo):
    B, S, D = h.shape
    M = m.shape[1]
    q = (h @ wq).reshape(B, S, XA_HEADS, XA_HEAD_DIM)
    k, v = jnp.split(m @ wkv, 2, axis=-1)
    k = k.reshape(B, M, XA_HEADS, XA_HEAD_DIM)
    v = v.reshape(B, M, XA_HEADS, XA_HEAD_DIM)
    s = jnp.einsum('bshd,bmhd->bhsm', q, k).astype(jnp.float32) * (XA_HEAD_DIM ** -0.5)
    p = jax.nn.softmax(s, axis=-1).astype(v.dtype)
    o = jnp.einsum('bhsm,bmhd->bshd', p, v).reshape(B, S, D)
    return o @ wo


def setup_inputs(seed: int = 0) -> dict:
    key = jax.random.key(seed)
    ks = jax.random.split(key, 32)
    f32 = jnp.float32

    def nrm(k, shape, fan_in):
        return jax.random.normal(k, shape, f32) * (fan_in ** -0.5)

    def gain(k, shape):
        return 1.0 + 0.05 * jax.random.normal(k, shape, f32)

    def bias(k, shape):
        return 0.01 * jax.random.normal(k, shape, f32)

    u = jax.random.uniform(ks[16], (DEPTH, LRU_WIDTH), f32, minval=0.9, maxval=0.999)
    p = u ** (1.0 / LRU_C)
    lam = jnp.log(p) - jnp.log1p(-p)
    return {
        'x': jax.random.normal(ks[0], (BATCH, SEQ, D_MODEL), f32),
        'mem': jax.random.normal(ks[1], (BATCH, N_MEM, D_MODEL), f32),
        'ffn1_norm': gain(ks[2], (DEPTH, D_MODEL)),
        'ffn1_w_up': nrm(ks[3], (DEPTH, D_MODEL, 2 * D_FF), D_MODEL),
        'ffn1_w_down': nrm(ks[4], (DEPTH, D_FF, D_MODEL), D_FF),
        'mix_norm': gain(ks[5], (DEPTH, D_MODEL)),
        'w_in': nrm(ks[6], (DEPTH, D_MODEL, IN_COLS), D_MODEL),
        'b_gate': bias(ks[7], (DEPTH, N_BRANCH, D_MODEL)),
        'hgrn_lb_logits': 0.5 * jax.random.normal(ks[8], (DEPTH + 1, HG_WIDTH), f32),
        'hgrn_norm': gain(ks[9], (DEPTH, HG_WIDTH)),
        'conv_w': 0.5 * jax.random.normal(ks[10], (DEPTH, CONV_WIDTH, LRU_WIDTH), f32),
        'conv_b': bias(ks[11], (DEPTH, LRU_WIDTH)),
        'lru_wa': nrm(ks[12], (DEPTH, LRU_BLOCKS, LRU_BLOCK, LRU_BLOCK), LRU_BLOCK),
        'lru_ba': bias(ks[13], (DEPTH, LRU_WIDTH)),
        'lru_wx': nrm(ks[14], (DEPTH, LRU_BLOCKS, LRU_BLOCK, LRU_BLOCK), LRU_BLOCK),
        'lru_bx': bias(ks[15], (DEPTH, LRU_WIDTH)),
        'lru_lambda': lam,
        'w_branch_a': nrm(ks[17], (DEPTH, HG_WIDTH, D_MODEL), HG_WIDTH),
        'w_branch_b': nrm(ks[18], (DEPTH, LRU_WIDTH, D_MODEL), LRU_WIDTH),
        'w_out': nrm(ks[19], (DEPTH, D_MODEL, D_MODEL), D_MODEL),
        'xattn_norm': gain(ks[20], (DEPTH, D_MODEL)),
        'mem_norm': gain(ks[21], (DEPTH, D_MODEL)),
        'xattn_wq': nrm(ks[22], (DEPTH, D_MODEL, D_MODEL), D_MODEL),
        'xattn_wkv': nrm(ks[23], (DEPTH, D_MODEL, 2 * D_MODEL), D_MODEL),
        'xattn_wo': nrm(ks[24], (DEPTH, D_MODEL, D_MODEL), D_MODEL),
        'ffn2_norm': gain(ks[25], (DEPTH, D_MODEL)),
        'ffn2_w_up': nrm(ks[26], (DEPTH, D_MODEL, 2 * D_FF), D_MODEL),
        'ffn2_w_down': nrm(ks[27], (DEPTH, D_FF, D_MODEL), D_FF),
        'final_norm': gain(ks[28], (D_MODEL,)),
    }


def reference(x, mem, ffn1_norm, ffn1_w_up, ffn1_w_down, mix_norm, w_in, b_gate,
              hgrn_lb_logits, hgrn_norm, conv_w, conv_b, lru_wa, lru_ba, lru_wx, lru_bx,
              lru_lambda, w_branch_a, w_branch_b, w_out, xattn_norm, mem_norm, xattn_wq,
              xattn_wkv, xattn_wo, ffn2_norm, ffn2_w_up, ffn2_w_down, final_norm):
    B, S, D = x.shape
    lb_table = jnp.cumsum(jax.nn.softmax(hgrn_lb_logits.astype(jnp.float32), axis=0), axis=0)
    splits = [HG_WIDTH, 2 * HG_WIDTH, 3 * HG_WIDTH, 4 * HG_WIDTH,
              4 * HG_WIDTH + LRU_WIDTH, 4 * HG_WIDTH + 2 * LRU_WIDTH]
    for l in range(DEPTH):
        x = x + 0.5 * swiglu(rms_norm(x, ffn1_norm[l]), ffn1_w_up[l], ffn1_w_down[l])

        h = rms_norm(x, mix_norm[l])
        proj = h @ w_in[l]
        q_a, f_a, i_a, g_a, x_b, gate_b, gates = jnp.split(proj, splits, axis=-1)

        lb = lb_table[l].reshape(HG_HEADS, HG_KDIM)
        o_a = hgrn2_chunked(q_a.reshape(B, S, HG_HEADS, HG_KDIM),
                            f_a.reshape(B, S, HG_HEADS, HG_KDIM),
                            i_a.reshape(B, S, HG_HEADS, HG_VDIM), lb)
        o_a = rms_norm(o_a, hgrn_norm[l].reshape(HG_HEADS, HG_VDIM))
        y_a = (o_a.reshape(B, S, HG_WIDTH) * jax.nn.silu(g_a.astype(jnp.float32))).astype(x.dtype)

        y_b = rglru_branch(x_b, gate_b, conv_w[l], conv_b[l], lru_wa[l], lru_ba[l],
                           lru_wx[l], lru_bx[l], lru_lambda[l])

        g = jax.nn.sigmoid(gates.reshape(B, S, N_BRANCH, D) + b_gate[l])
        merged = g[:, :, 0, :] * (y_a @ w_branch_a[l]) + g[:, :, 1, :] * (y_b @ w_branch_b[l])
        x = x + merged @ w_out[l]

        x = x + cross_attention(rms_norm(x, xattn_norm[l]), rms_norm(mem, mem_norm[l]),
                                xattn_wq[l], xattn_wkv[l], xattn_wo[l])

        x = x + 0.5 * swiglu(rms_norm(x, ffn2_norm[l]), ffn2_w_up[l], ffn2_w_down[l])
    return rms_norm(x, final_norm)
```

```python
import contextlib
import numpy as np
import concourse.bass as bass
import concourse.mybir as mybir
from concourse.bass_utils import run_bass_kernel_spmd

F32 = mybir.dt.float32
BF16 = mybir.dt.bfloat16
ALU = mybir.AluOpType
AF = mybir.ActivationFunctionType
AX = mybir.AxisListType

ENGS = ['pe', 'act', 'dve', 'pool', 'sp']
SEM_LIMIT = 16000


class Buf:
    __slots__ = ('w', 'r', 'name')

    def __init__(self, name=''):
        self.w = None
        self.r = {}
        self.name = name


class DmaSem:
    def __init__(self, prog, name):
        self.prog = prog
        self.name = name
        self.epoch = 0
        self._new()

    def _new(self):
        self.key = '%s#%d' % (self.name, self.epoch)
        self.h = self.prog.stack.enter_context(self.prog.nc.semaphore(self.key.replace('#', '_')))
        self.prog.sem_by_key[self.key] = self.h
        self.cnt = 0
        self.epoch += 1

    def bump(self):
        if self.cnt + 16 > SEM_LIMIT:
            self._new()
        self.cnt += 16
        return self.h


class Prog:
    def __init__(self, nc, stack):
        self.nc = nc
        self.stack = stack
        self.ops = {e: [] for e in ENGS}
        self.sem_by_key = {}
        self.key = {}
        self.semh = {}
        self.cnt = {}
        self.epoch = {e: 0 for e in ENGS}
        for e in ENGS:
            self._new_epoch(e)
        self.pending = {e: False for e in ENGS}
        self.waited = {e: {} for e in ENGS}
        self.ndsem = 0
        self.nops = 0

    def _new_epoch(self, e):
        k = '%s#%d' % (e, self.epoch[e])
        self.epoch[e] += 1
        h = self.stack.enter_context(self.nc.semaphore('cs_' + k.replace('#', '_')))
        self.key[e] = k
        self.semh[e] = h
        self.sem_by_key[k] = h
        self.cnt[e] = 0

    def dma_sem(self, name=None):
        self.ndsem += 1
        return DmaSem(self, name or ('dma%d' % self.ndsem))

    @staticmethod
    def _acc(need, k, v):
        if need.get(k, 0) < v:
            need[k] = v

    def _deps(self, reads, writes):
        need = {}
        for b in reads:
            if b.w is not None:
                self._acc(need, b.w[0], b.w[1])
        for b in writes:
            if b.w is not None:
                self._acc(need, b.w[0], b.w[1])
            for k, v in b.r.items():
                self._acc(need, k, v)
        return need

    def _waits(self, eng, need):
        waits = []
        wd = self.waited[eng]
        for k, v in need.items():
            if k.split('#')[0] == eng:
                if eng == 'pe':
                    continue
                if k == self.key[eng] and v > self.cnt[eng]:
                    continue
            if wd.get(k, 0) < v:
                wd[k] = v
                waits.append((self.sem_by_key[k], v, k))
        return waits

    def _commit(self, tok, reads, writes):
        for b in reads:
            self._acc(b.r, tok[0], tok[1])
        for b in writes:
            b.w = tok
            b.r = {}

    def op(self, eng, fn, reads=(), writes=(), signal=True):
        need = self._deps(reads, writes)
        waits = self._waits(eng, need)
        if signal:
            if self.cnt[eng] + 1 > SEM_LIMIT and not self.pending[eng]:
                self._new_epoch(eng)
            self.cnt[eng] += 1
            tok = (self.key[eng], self.cnt[eng])
            inc = (self.semh[eng], 1, self.key[eng])
            self.pending[eng] = False
        else:
            tok = (self.key[eng], self.cnt[eng] + 1)
            inc = None
            self.pending[eng] = True
        self.ops[eng].append((fn, waits, inc))
        self._commit(tok, reads, writes)
        self.nops += 1
        return tok

    def dma(self, eng, fns, dsem, reads=(), writes=()):
        need = self._deps(reads, writes)
        waits = self._waits(eng, need)
        if dsem.cnt + 16 * len(fns) > SEM_LIMIT:
            dsem._new()
        for i, fn in enumerate(fns):
            h = dsem.bump()
            self.ops[eng].append((fn, waits if i == 0 else [], (h, 16, dsem.key)))
        tok = (dsem.key, dsem.cnt)
        self._commit(tok, reads, writes)
        self.nops += len(fns)
        return tok

    def wait_tok(self, eng, tok):
        waits = self._waits(eng, {tok[0]: tok[1]})
        if waits:
            self.ops[eng].append((None, waits, None))

    def barrier(self, engs=('pe', 'act', 'dve')):
        for e in engs:
            assert not self.pending[e], e
        for e in engs:
            need = {self.key[o]: self.cnt[o] for o in engs if o != e and self.cnt[o] > 0}
            waits = self._waits(e, need)
            if waits:
                self.ops[e].append((None, waits, None))

    def check(self):
        vals = {}
        ptr = {e: 0 for e in ENGS}
        progress = True
        while progress:
            progress = False
            for e in ENGS:
                ops = self.ops[e]
                i = ptr[e]
                while i < len(ops):
                    fn, waits, inc = ops[i]
                    if any(vals.get(k, 0) < v for (_, v, k) in waits):
                        break
                    if inc is not None:
                        vals[inc[2]] = vals.get(inc[2], 0) + inc[1]
                    i += 1
                    progress = True
                ptr[e] = i
        bad = [e for e in ENGS if ptr[e] < len(self.ops[e])]
        if bad:
            for e in bad:
                fn, waits, inc = self.ops[e][ptr[e]]
                print("DEADLOCK", e, ptr[e], len(self.ops[e]), [(k, v, vals.get(k, 0)) for (_, v, k) in waits])
            raise RuntimeError("static deadlock check failed")

    def emit(self):
        nc = self.nc
        for e in ENGS:
            assert not self.pending[e], e
        self.check()
        prog = self

        def replay(name, e):
            for (fn, waits, inc) in prog.ops[name]:
                for (semh, val, _) in waits:
                    e.wait_ge(semh, val)
                if fn is not None:
                    ins = fn(e)
                    if inc is not None:
                        ins.then_inc(inc[0], inc[1])

        with nc.Block() as blk:
            @blk.tensor
            def _(e):
                replay('pe', e)

            @blk.scalar
            def _(e):
                replay('act', e)

            @blk.vector
            def _(e):
                replay('dve', e)

            @blk.gpsimd
            def _(e):
                replay('pool', e)

            @blk.sync
            def _(e):
                replay('sp', e)


D = 2048
S = 4096
TT = 512
DFF = 5632
NM = 256
EPS = 1e-6
NSLOT = 3
HG_ORDER = (0, 1)
SLOT_ELEMS = 16 * 512

PC = {}
_c = 0
for _n, _w in [('ffn1_norm', 16), ('mix_norm', 16), ('xattn_norm', 16), ('mem_norm', 16), ('ffn2_norm', 16),
               ('final_norm', 16), ('bgA', 16), ('bgB', 16), ('lb0', 8), ('lb1', 8), ('hgrn_norm', 8),
               ('conv_w', 32), ('conv_b', 8), ('lru_ba', 8), ('lru_bx', 8), ('lru_lambda', 8)]:
    PC[_n] = _c
    _c += _w
NPAR = _c


def pack_params(inp):
    def cols(v):
        v = np.asarray(v, dtype=np.float32).reshape(-1, 128)
        return v.T
    par = np.zeros((128, NPAR), dtype=np.float32)

    def put(name, arr):
        a = cols(arr)
        par[:, PC[name]:PC[name] + a.shape[1]] = a
    put('ffn1_norm', inp['ffn1_norm'][0]); put('mix_norm', inp['mix_norm'][0])
    put('xattn_norm', inp['xattn_norm'][0]); put('mem_norm', inp['mem_norm'][0])
    put('ffn2_norm', inp['ffn2_norm'][0]); put('final_norm', inp['final_norm'])
    put('bgA', inp['b_gate'][0, 0]); put('bgB', inp['b_gate'][0, 1])
    put('lb0', inp['hgrn_lb_logits'][0]); put('lb1', inp['hgrn_lb_logits'][1])
    put('hgrn_norm', inp['hgrn_norm'][0])
    cw = np.asarray(inp['conv_w'][0], dtype=np.float32)
    for j in range(4):
        par[:, PC['conv_w'] + j * 8:PC['conv_w'] + j * 8 + 8] = cols(cw[j])
    put('conv_b', inp['conv_b'][0]); put('lru_ba', inp['lru_ba'][0]); put('lru_bx', inp['lru_bx'][0])
    put('lru_lambda', inp['lru_lambda'][0])
    return par


def make_consts():
    ident = np.eye(128, dtype=np.float32)
    m = np.triu(np.ones((64, 64), dtype=np.float32))
    maskT = np.tile(m, (1, 4))
    maskT = np.concatenate([maskT, np.zeros_like(maskT)], 0)
    sm = np.ones((128, 512), dtype=np.float32)
    sm[:, ::64] = 0.0
    return ident, maskT, sm


class Arena:
    def __init__(self, nc, nbytes):
        self.t = nc.alloc_sbuf_tensor("arena", [128, nbytes // 2], BF16)
        self.cap = nbytes
        self.off = 0

    def alloc(self, free_shape, dtype, name=''):
        n = 1
        for s_ in free_shape:
            n *= s_
        sz = 4 if dtype == F32 else 2
        nb = (n * sz + 31) // 32 * 32
        assert self.off + nb <= self.cap, ("arena overflow", name, self.off, nb, self.cap)
        v = self.t[:, self.off // 2:(self.off + n * sz) // 2]
        self.off += nb
        if dtype == F32:
            v = v.bitcast(F32)
        if len(free_shape) == 2:
            v = v.rearrange("p (a b) -> p a b", a=free_shape[0])
        elif len(free_shape) == 3:
            v = v.rearrange("p (a b c) -> p a b c", a=free_shape[0], b=free_shape[1])
        return v

    def mark(self):
        return self.off

    def release(self, m):
        self.off = m


def build_program(ntiles=8, stop_after=None):
    nc = bass.Bass("TRN2", target_bir_lowering=False)
    dram = {}

    def din(name, shape):
        dram[name] = nc.dram_tensor(name, list(shape), F32, kind="ExternalInput").ap()
        return dram[name]
    xT_d = din("xT", [D, S])
    memT_d = din("memT", [D, NM])
    par_d = din("par", [128, NPAR])
    ident_d = din("ident", [128, 128])
    maskT_d = din("maskT", [128, 256])
    scanm_d = din("scanm", [128, 512])
    W = {}
    for name, shape in [('ffn1_w_up', [D, 2 * DFF]), ('ffn1_w_down', [DFF, D]), ('w_in', [D, 10240]),
                        ('lru_wa', [8, 128, 128]), ('lru_wx', [8, 128, 128]), ('w_branch_a', [1024, D]),
                        ('w_branch_b', [1024, D]), ('w_out', [D, D]), ('xattn_wq', [D, D]), ('xattn_wkv', [D, 2 * D]),
                        ('xattn_wo', [D, D]), ('ffn2_w_up', [D, 2 * DFF]), ('ffn2_w_down', [DFF, D])]:
        W[name] = din(name, shape)
    outT_d = nc.dram_tensor("outT", [D, S], F32, kind="ExternalOutput").ap()

    def kview(ap):
        return ap.rearrange("(kc p) f -> p kc f", p=128)

    with contextlib.ExitStack() as st:
        P = Prog(nc, st)
        xT = nc.alloc_sbuf_tensor("xTs", [128, 16, TT], F32)
        hT = nc.alloc_sbuf_tensor("hTs", [128, 16, TT], BF16)
        KT = nc.alloc_sbuf_tensor("KTs", [128, 16, NM], BF16)
        Vt = nc.alloc_sbuf_tensor("Vts", [128, 2, D], BF16)
        state = nc.alloc_sbuf_tensor("state", [128, 8, 128], F32)
        state_r = nc.alloc_sbuf_tensor("state_r", [128, 4, 128], BF16)
        par = nc.alloc_sbuf_tensor("par_s", [128, NPAR], F32)
        dpar = nc.alloc_sbuf_tensor("dpar", [128, 48], F32)
        ident = nc.alloc_sbuf_tensor("ident_s", [128, 128], BF16)
        ones = nc.alloc_sbuf_tensor("ones", [128, 128], BF16)
        maskT = nc.alloc_sbuf_tensor("maskT_s", [128, 256], F32)
        scanm = nc.alloc_sbuf_tensor("scanm_s", [128, 512], F32)
        lruw = nc.alloc_sbuf_tensor("lruw", [128, 2, 8, 128], BF16)
        halo = nc.alloc_sbuf_tensor("halo", [128, 8, 4], F32)
        lru_st = nc.alloc_sbuf_tensor("lru_st", [128, 8], F32)
        zero1 = nc.alloc_sbuf_tensor("zero1", [128, 1], F32)
        slots = [nc.alloc_sbuf_tensor("wslot%d" % i, [128, SLOT_ELEMS], BF16) for i in range(NSLOT)]
        arena_bytes = (nc.sbuf_bytes_remaining - 1024) // 64 * 64
        arena = Arena(nc, arena_bytes)
        psum = [nc.alloc_psum_tensor("ps%d" % i, [128, 512], F32) for i in range(8)]

        x_b = [Buf('x%d' % i) for i in range(16)]
        hT_b = Buf('hT')
        KT_b, V_b = Buf('KT'), Buf('V')
        state_b = [Buf('st%d' % i) for i in range(8)]
        stater_b = [Buf('str%d' % i) for i in range(4)]
        par_b, dpar_b, const_b, lruw_b = Buf('par'), Buf('dpar'), Buf('const'), Buf('lruw')
        halo_b = [Buf('halo%d' % i) for i in range(8)]
        lrust_b = [Buf('lrust%d' % i) for i in range(8)]
        slot_b = [Buf('slot%d' % i) for i in range(NSLOT)]
        slot_sem = [P.dma_sem('wsl%d' % i) for i in range(NSLOT)]
        ps_b = [Buf('ps%d' % i) for i in range(8)]
        x_sem = P.dma_sem('xld')
        o_sem = P.dma_sem('ost')
        psi = [0]
        dbg_list = []

        def dbg(name, ap, buf, eng='pool'):
            if stop_after is None or not stop_after.startswith('dbg'):
                return
            p_, n_ = ap.shape[0], 1
            for s_ in ap.shape[1:]:
                n_ *= s_
            dt_ = nc.dram_tensor("dbg_" + name, [p_] + list(ap.shape[1:]), F32, kind="ExternalOutput").ap()
            sem_ = P.dma_sem('dbg_' + name)
            tok = P.dma(eng, [lambda e: e.dma_start(out=dt_, in_=ap)], sem_, reads=[buf])
            dbg_list.append(tok)

        def bank():
            i = psi[0] % 8
            psi[0] += 1
            return psum[i], ps_b[i]

        items = []

        def add_item(*subs):
            items.append(list(subs))

        def ffn_items(wu, wd):
            wuv = kview(wu)
            wdv = kview(wd)
            for j2 in range(22):
                add_item((wuv[:, :, j2 * 256:(j2 + 1) * 256], 16, 256), (wuv[:, :, DFF + j2 * 256:DFF + (j2 + 1) * 256], 16, 256))
            for d in range(16):
                add_item((wdv[:, :, d * 128:(d + 1) * 128], 44, 128))

        def tile_items():
            ffn_items(W['ffn1_w_up'], W['ffn1_w_down'])
            win = kview(W['w_in'])
            for hg in HG_ORDER:
                for base in (0, 1024, 2048, 3072):
                    add_item((win[:, :, base + hg * 512:base + (hg + 1) * 512], 16, 512))
            for g2 in range(4):
                add_item((win[:, :, 4096 + g2 * 256:4096 + (g2 + 1) * 256], 16, 256), (win[:, :, 5120 + g2 * 256:5120 + (g2 + 1) * 256], 16, 256))
            wba = kview(W['w_branch_a'])
            wbb = kview(W['w_branch_b'])
            for d in range(16):
                add_item((win[:, :, 6144 + d * 128:6144 + (d + 1) * 128], 16, 128), (win[:, :, 8192 + d * 128:8192 + (d + 1) * 128], 16, 128),
                         (wba[:, :, d * 128:(d + 1) * 128], 8, 128), (wbb[:, :, d * 128:(d + 1) * 128], 8, 128))
            for name in ('w_out', 'xattn_wq', 'xattn_wo'):
                wv = kview(W[name])
                for dg in range(4):
                    add_item((wv[:, :, dg * 512:(dg + 1) * 512], 16, 512))
            ffn_items(W['ffn2_w_up'], W['ffn2_w_down'])

        wkv = kview(W['xattn_wkv'])
        for g in range(4):
            add_item((wkv[:, :, g * 512:(g + 1) * 512], 16, 512))
        for g in range(4):
            add_item((wkv[:, :, D + g * 512:D + (g + 1) * 512], 16, 512))
        for _ in range(ntiles):
            tile_items()
        wptr = {'issued': 0, 'next': 0}

        def slot_views(s, subs):
            off = 0
            views = []
            for (src, nk, ncl) in subs:
                views.append(slots[s][:, off:off + nk * ncl].rearrange("p (k c) -> p k c", k=nk))
                off += nk * ncl
            assert off <= SLOT_ELEMS
            return views

        def issue_one():
            i = wptr['issued']
            if i >= len(items):
                return
            subs = items[i]
            s = i % NSLOT
            views = slot_views(s, subs)
            fns = []
            for (src, nk, ncl), dst in zip(subs, views):
                n = nk * ncl
                npieces = 4 if n >= 8192 else (2 if n >= 4096 else 1)
                step = nk // npieces
                for g in range(npieces):
                    k0 = g * step
                    k1 = (g + 1) * step if g < npieces - 1 else nk
                    fns.append(lambda e, dst=dst, src=src, k0=k0, k1=k1: e.dma_start(out=dst[:, k0:k1, :], in_=src[:, k0:k1, :]))
            P.dma('pool', fns, slot_sem[s], writes=[slot_b[s]])
            wptr['issued'] += 1

        def next_w():
            i = wptr['next']
            while wptr['issued'] < min(i + NSLOT, len(items)):
                issue_one()
            s = i % NSLOT
            wptr['next'] += 1
            v = slot_views(s, items[i])
            return (v[0] if len(v) == 1 else v), slot_b[s]

        def mm_group(bk, bkb, out_ap, pairs, reads):
            n = len(pairs)
            for i, (l, r) in enumerate(pairs):
                last = (i == n - 1)
                P.op('pe', lambda e, l=l, r=r, i=i, last=last: e.matmul(out_ap, lhsT=l, rhs=r, start=(i == 0), stop=last),
                     reads=reads, writes=[bkb] if (i == 0 or last) else [], signal=last)

        def rstd_from(bk_view, bkb, dst, dst_b, inv_n):
            P.op('act', lambda e: e.activation(out=dst, in_=bk_view, func=AF.Sqrt, scale=inv_n, bias=EPS), reads=[bkb], writes=[dst_b])
            P.op('dve', lambda e: e.reciprocal(out=dst, in_=dst), reads=[dst_b], writes=[dst_b])

        def rmsnorm_to_hT(gcol):
            m = arena.mark()
            sq = arena.alloc([16, TT], BF16, 'sq')
            rstd = arena.alloc([TT], F32, 'rstd')
            sq_b, rstd_b = Buf('sq'), Buf('rstd')
            for kc in range(16):
                P.op('act', lambda e, kc=kc: e.activation(out=sq[:, kc, :], in_=xT[:, kc, :], func=AF.Square),
                     reads=[x_b[kc]], writes=[sq_b])
            bk, bkb = bank()
            mm_group(bk, bkb, bk[:, :], [(ones[:, :], sq[:, kc, :]) for kc in range(16)], [sq_b, const_b])
            rstd_from(bk[:, :], bkb, rstd, rstd_b, 1.0 / D)
            for kc in range(16):
                P.op('dve', lambda e, kc=kc: e.scalar_tensor_tensor(out=hT[:, kc, :], in0=xT[:, kc, :], scalar=par[:, gcol + kc:gcol + kc + 1],
                                                                    in1=rstd, op0=ALU.mult, op1=ALU.mult),
                     reads=[x_b[kc], rstd_b, par_b], writes=[hT_b])
            P.barrier()
            arena.release(m)

        def ffn():
            m = arena.mark()
            hid = arena.alloc([44, TT], BF16, 'hid')
            sgt = [arena.alloc([TT], F32, 'sg%d' % i) for i in range(2)]
            hid_b = [Buf('hid%d' % j) for j in range(44)]
            sgt_b = [Buf('sgt0'), Buf('sgt1')]
            for j2 in range(22):
                (wg, wu), wgb = next_w()
                wub = wgb
                for jj in range(2):
                    j = j2 * 2 + jj
                    bg, bgb = bank()
                    mm_group(bg, bgb, bg[:, :], [(wg[:, kc, jj * 128:(jj + 1) * 128], hT[:, kc, :]) for kc in range(16)], [wgb, hT_b])
                    bu, bub = bank()
                    mm_group(bu, bub, bu[:, :], [(wu[:, kc, jj * 128:(jj + 1) * 128], hT[:, kc, :]) for kc in range(16)], [wub, hT_b])
                    sg, sgb = sgt[j % 2], sgt_b[j % 2]
                    P.op('act', lambda e, sg=sg, bg=bg: e.activation(out=sg, in_=bg[:, :], func=AF.Silu), reads=[bgb], writes=[sgb])
                    P.op('dve', lambda e, sg=sg, bu=bu, j=j: e.tensor_tensor(out=hid[:, j, :], in0=sg, in1=bu[:, :], op=ALU.mult),
                         reads=[sgb, bub], writes=[hid_b[j]])
            for d in range(16):
                wd, wdb = next_w()
                bk, bkb = bank()
                mm_group(bk, bkb, bk[:, :], [(wd[:, kc, :], hid[:, kc, :]) for kc in range(44)], [wdb] + hid_b)
                P.op('dve', lambda e, d=d, bk=bk: e.scalar_tensor_tensor(out=xT[:, d, :], in0=bk[:, :], scalar=0.5, in1=xT[:, d, :],
                                                                         op0=ALU.mult, op1=ALU.add),
                     reads=[bkb, x_b[d]], writes=[x_b[d]])
            P.barrier()
            arena.release(m)

        def proj4(w, wb, jj, bk, bkb):
            mm_group(bk, bkb, bk[:, :], [(w[:, kc, jj * 128:(jj + 1) * 128], hT[:, kc, :]) for kc in range(16)], [wb, hT_b])

        def mixer(tile):
            m0 = arena.mark()
            ya = arena.alloc([8, TT], BF16, 'ya')
            yb = arena.alloc([8, TT], BF16, 'yb')
            ya_b = [Buf('ya%d' % i) for i in range(8)]
            yb_b = [Buf('yb%d' % i) for i in range(8)]
            for hg in HG_ORDER:
                m1 = arena.mark()
                T1 = arena.alloc([4, TT], F32, 'T1'); T2 = arena.alloc([4, TT], F32, 'T2'); T3 = arena.alloc([4, TT], F32, 'T3')
                T4 = arena.alloc([4, TT], F32, 'T4'); T5 = arena.alloc([4, TT], F32, 'T5')
                qt = arena.alloc([4, TT], BF16, 'qt'); kt = arena.alloc([4, TT], BF16, 'kt'); kd = arena.alloc([4, TT], BF16, 'kd')
                vt = arena.alloc([8, 512], BF16, 'vt')
                scT = [arena.alloc([4, 64], BF16, 'scT%d' % i) for i in range(2)]
                kdT = [arena.alloc([4, 128], BF16, 'kdT%d' % i) for i in range(2)]
                sml = arena.alloc([3, 4, 8], F32, 'sml')
                T1b, T2b, T3b, T4b, T5b = Buf('T1'), Buf('T2'), Buf('T3'), Buf('T4'), Buf('T5')
                qtb, ktb, kdb, vtb = Buf('qt'), Buf('kt'), Buf('kd'), Buf('vt')
                scTb = [Buf('scT0'), Buf('scT1')]; kdTb = [Buf('kdT0'), Buf('kdT1')]
                smlb = Buf('sml')
                osqb, rstb = T3b, T5b
                osq = T3.bitcast(BF16)[:, :, 0:TT]
                rst = T5[:, 0, :]
                lbc = 0 + hg * 4

                wq_, wqb = next_w()
                for jj in range(4):
                    bk, bkb = bank()
                    proj4(wq_, wqb, jj, bk, bkb)
                    P.op('act', lambda e, jj=jj, bk=bk: e.activation(out=T1[:, jj, :], in_=bk[:, :], func=AF.Silu), reads=[bkb], writes=[T1b])
                wf_, wfb = next_w()
                for jj in range(4):
                    bk, bkb = bank()
                    proj4(wf_, wfb, jj, bk, bkb)
                    P.op('act', lambda e, jj=jj, bk=bk: e.activation(out=T2[:, jj, :], in_=bk[:, :], func=AF.Sigmoid), reads=[bkb], writes=[T2b])
                wv_, wvb = next_w()
                for c in range(8):
                    bk, bkb = bank()
                    mm_group(bk, bkb, bk[0:64, :], [(hT[:, kc, c * 64:(c + 1) * 64], wv_[:, kc, :]) for kc in range(16)], [wvb, hT_b])
                    P.op('act', lambda e, c=c, bk=bk: e.activation(out=vt[0:64, c, :], in_=bk[0:64, :], func=AF.Copy), reads=[bkb], writes=[vtb])
                for jj in range(4):
                    P.op('act', lambda e, jj=jj, lbc=lbc: e.activation(out=T3[:, jj, :], in_=T2[:, jj, :], func=AF.Ln,
                                                              scale=dpar[:, 8 + lbc + jj:9 + lbc + jj], bias=dpar[:, lbc + jj:lbc + jj + 1]),
                         reads=[T2b, dpar_b], writes=[T3b])
                for jj in range(4):
                    P.op('dve', lambda e, jj=jj, lbc=lbc: e.tensor_scalar(out=T2[:, jj, :], in0=T2[:, jj, :], scalar1=dpar[:, 16 + lbc + jj:17 + lbc + jj],
                                                                 scalar2=dpar[:, 8 + lbc + jj:9 + lbc + jj], op0=ALU.mult, op1=ALU.add),
                         reads=[T2b, T3b, dpar_b], writes=[T2b])
                for jj in range(4):
                    P.op('dve', lambda e, jj=jj: e.tensor_tensor_scan(out=T4[:, jj, :], data0=scanm[:, :], data1=T3[:, jj, :], initial=zero1[:, 0:1],
                                                                      op0=ALU.mult, op1=ALU.add),
                         reads=[T3b, const_b], writes=[T4b])
                b4 = T4.rearrange("p h (c t) -> p h c t", t=64)
                r_ap = b4[:, :, :, 31:32]
                bl_ap = b4[:, :, :, 63:64]
                P.op('dve', lambda e: e.tensor_tensor(out=T3.rearrange("p h (c t) -> p h c t", t=64), in0=b4,
                                                      in1=r_ap.broadcast_to([128, 4, 8, 64]), op=ALU.subtract),
                     reads=[T4b, T3b], writes=[T3b])
                P.op('act', lambda e: e.activation(out=T5, in_=T3, func=AF.Exp), reads=[T3b], writes=[T5b])
                P.op('dve', lambda e: e.tensor_tensor(out=qt, in0=T1, in1=T5, op=ALU.mult), reads=[T1b, T5b], writes=[qtb])
                P.op('act', lambda e: e.activation(out=T5, in_=T3, func=AF.Exp, scale=-1.0), reads=[T3b, qtb], writes=[T5b])
                P.op('dve', lambda e: e.tensor_tensor(out=kt, in0=T2, in1=T5, op=ALU.mult), reads=[T2b, T5b], writes=[ktb])
                P.op('dve', lambda e: e.tensor_tensor(out=T3.rearrange("p h (c t) -> p h c t", t=64), in0=b4,
                                                      in1=bl_ap.broadcast_to([128, 4, 8, 64]), op=ALU.subtract),
                     reads=[T4b, T3b, T5b], writes=[T3b])
                P.op('act', lambda e: e.activation(out=T5, in_=T3, func=AF.Exp, scale=-1.0), reads=[T3b, ktb], writes=[T5b])
                P.op('dve', lambda e: e.tensor_tensor(out=kd, in0=T2, in1=T5, op=ALU.mult), reads=[T2b, T5b], writes=[kdb])
                P.op('act', lambda e: e.activation(out=sml[:, 0, :, :], in_=b4[:, :, :, 31], func=AF.Exp), reads=[T4b], writes=[smlb])
                P.op('act', lambda e: e.activation(out=sml[:, 1, :, :], in_=b4[:, :, :, 63], func=AF.Exp), reads=[T4b], writes=[smlb])
                wg_, wgb = next_w()
                for jj in range(4):
                    bk, bkb = bank()
                    proj4(wg_, wgb, jj, bk, bkb)
                    P.op('act', lambda e, jj=jj, bk=bk: e.activation(out=T1[:, jj, :], in_=bk[:, :], func=AF.Silu), reads=[bkb, qtb], writes=[T1b])
                dbg('qt%d' % hg, qt, qtb); dbg('kt%d' % hg, kt, ktb); dbg('kd%d' % hg, kd, kdb); dbg('vt%d' % hg, vt[0:64], vtb)
                dbg('gs%d' % hg, T1, T1b); dbg('b%d' % hg, T4, T4b); dbg('sml%d' % hg, sml, smlb)
                o_ = T2
                ob = T2b
                for c in range(8):
                    cs = slice(c * 64, (c + 1) * 64)
                    pp = c % 2
                    bS, bSb = bank()
                    for hh in range(4):
                        P.op('pe', lambda e, hh=hh, cs=cs, bS=bS: e.matmul(bS[0:64, hh * 64:(hh + 1) * 64], lhsT=kt[:, hh, cs], rhs=qt[:, hh, cs], start=True, stop=True),
                             reads=[ktb, qtb], writes=[bSb] if hh in (0, 3) else [], signal=(hh == 3))
                    P.op('dve', lambda e, pp=pp, bS=bS: e.tensor_tensor(out=scT[pp][0:64, :, :], in0=bS[0:64, 0:256].rearrange("p (h t) -> p h t", h=4),
                                                                      in1=maskT[0:64, :].rearrange("p (h t) -> p h t", h=4), op=ALU.mult),
                         reads=[bSb, const_b], writes=[scTb[pp]])
                    bT, bTb = bank()
                    for hh in range(4):
                        P.op('pe', lambda e, hh=hh, cs=cs, bT=bT: e.matmul(bT[0:64, hh * 128:(hh + 1) * 128], lhsT=kd[:, hh, cs], rhs=ident[:, :], start=True, stop=True),
                             reads=[kdb, const_b], writes=[bTb] if hh in (0, 3) else [], signal=(hh == 3))
                    P.op('act', lambda e, pp=pp, bT=bT: e.activation(out=kdT[pp][0:64, :, :], in_=bT[0:64, :].rearrange("p (h k) -> p h k", h=4), func=AF.Copy),
                         reads=[bTb], writes=[kdTb[pp]])
                    for hh in range(4):
                        h = hg * 4 + hh
                        P.op('act', lambda e, hh=hh, h=h, c=c: e.activation(out=state_r[:, hh, :], in_=state[:, h, :], func=AF.Copy,
                                                                            scale=sml[:, 0, hh, c:c + 1]),
                             reads=[state_b[h], smlb], writes=[stater_b[hh]])
                    bO, bOb = bank()
                    for hh in range(4):
                        P.op('pe', lambda e, hh=hh, cs=cs, bO=bO: e.matmul(bO[:, hh * 64:(hh + 1) * 64], lhsT=state_r[:, hh, :], rhs=qt[:, hh, cs], start=True, stop=False),
                             reads=[stater_b[hh], qtb], writes=[bOb] if hh == 0 else [], signal=False)
                        P.op('pe', lambda e, hh=hh, c=c, pp=pp, bO=bO: e.matmul(bO[:, hh * 64:(hh + 1) * 64], lhsT=vt[0:64, c, hh * 128:(hh + 1) * 128],
                                                                             rhs=scT[pp][0:64, hh, :], start=False, stop=True),
                             reads=[vtb, scTb[pp]], writes=[bOb] if hh == 3 else [], signal=(hh == 3))
                    P.op('act', lambda e, cs=cs, bO=bO: e.activation(out=o_[:, :, cs], in_=bO[:, 0:256].rearrange("p (h t) -> p h t", h=4), func=AF.Copy),
                         reads=[bOb, kdb], writes=[ob])
                    bU, bUb = bank()
                    for hh in range(4):
                        P.op('pe', lambda e, hh=hh, c=c, pp=pp, bU=bU: e.matmul(bU[:, hh * 128:(hh + 1) * 128], lhsT=kdT[pp][0:64, hh, :],
                                                                             rhs=vt[0:64, c, hh * 128:(hh + 1) * 128], start=True, stop=True),
                             reads=[kdTb[pp], vtb], writes=[bUb] if hh in (0, 3) else [], signal=(hh == 3))
                    for hh in range(4):
                        h = hg * 4 + hh
                        P.op('dve', lambda e, hh=hh, h=h, c=c, bU=bU: e.scalar_tensor_tensor(out=state[:, h, :], in0=state[:, h, :], scalar=sml[:, 1, hh, c:c + 1],
                                                                                          in1=bU[:, hh * 128:(hh + 1) * 128], op0=ALU.mult, op1=ALU.add),
                             reads=[bUb, smlb, state_b[h], stater_b[hh]], writes=[state_b[h]])
                dbg('o%d' % hg, o_, ob)
                P.op('act', lambda e: e.activation(out=osq, in_=o_, func=AF.Square), reads=[ob], writes=[osqb])
                for hh in range(4):
                    h = hg * 4 + hh
                    bk, bkb = bank()
                    mm_group(bk, bkb, bk[:, :], [(ones[:, :], osq[:, hh, :])], [osqb, const_b])
                    rstd_from(bk[:, :], bkb, rst, rstb, 1.0 / 128)
                    P.op('dve', lambda e, hh=hh, h=h: e.scalar_tensor_tensor(out=o_[:, hh, :], in0=o_[:, hh, :], scalar=par[:, PC['hgrn_norm'] + h:PC['hgrn_norm'] + h + 1],
                                                                             in1=rst, op0=ALU.mult, op1=ALU.mult),
                         reads=[ob, rstb, par_b, osqb], writes=[ob])
                    P.op('dve', lambda e, hh=hh, h=h: e.tensor_tensor(out=ya[:, h, :], in0=o_[:, hh, :], in1=T1[:, hh, :], op=ALU.mult),
                         reads=[ob, T1b], writes=[ya_b[h]])
                P.barrier()
                arena.release(m1)
            if stop_after == 'hgrn':
                pass
            m1 = arena.mark()
            NSET = 2
            tmps = []
            for i in range(NSET):
                d_ = {}
                d_['xe'] = arena.alloc([TT + 4], F32, 'xe')
                for nm in ('xc', 'r', 'i', 'a', 'u', 'hh', 'xg', 't'):
                    d_[nm] = arena.alloc([TT], F32, nm)
                d_['xcb'] = arena.alloc([TT], BF16, 'xcb')
                d_['b'] = {k: Buf(k) for k in ('xe', 'xc', 'r', 'i', 'a', 'u', 'hh', 'xg', 't', 'xcb')}
                tmps.append(d_)
            cw, cb_ = PC['conv_w'], PC['conv_b']
            for g2 in range(4):
                (wx_, wgb_), wxb = next_w()
                wgbb = wxb
                for jj in range(2):
                    n = g2 * 2 + jj
                    t_ = tmps[n % NSET]
                    B = t_['b']
                    xe, xc, r_, i_, a_, u_, hh_, xg, tt, xcb = (t_[k] for k in ('xe', 'xc', 'r', 'i', 'a', 'u', 'hh', 'xg', 't', 'xcb'))
                    bk, bkb = bank()
                    proj4(wx_, wxb, jj, bk, bkb)
                    P.op('act', lambda e, xe=xe, bk=bk: e.activation(out=xe[:, 4:4 + TT], in_=bk[:, :], func=AF.Copy), reads=[bkb], writes=[B['xe']])
                    P.op('dve', lambda e, xe=xe, n=n: e.tensor_copy(out=xe[:, 0:4], in_=halo[:, n, :]), reads=[halo_b[n], B['xe']], writes=[B['xe']])
                    P.op('dve', lambda e, xe=xe, xc=xc, n=n: e.tensor_scalar(out=xc, in0=xe[:, 4:4 + TT], scalar1=par[:, cw + n:cw + n + 1], scalar2=par[:, cb_ + n:cb_ + n + 1],
                                                                         op0=ALU.mult, op1=ALU.add), reads=[B['xe'], par_b], writes=[B['xc']])
                    for j in (1, 2, 3):
                        P.op('dve', lambda e, xe=xe, xc=xc, n=n, j=j: e.scalar_tensor_tensor(out=xc, in0=xe[:, 4 - j:4 - j + TT], scalar=par[:, cw + j * 8 + n:cw + j * 8 + n + 1],
                                                                                          in1=xc, op0=ALU.mult, op1=ALU.add), reads=[B['xe'], B['xc'], par_b], writes=[B['xc']])
                    P.op('dve', lambda e, xe=xe, n=n: e.tensor_copy(out=halo[:, n, :], in_=xe[:, TT:TT + 4]), reads=[B['xe']], writes=[halo_b[n]])
                    P.op('act', lambda e, xc=xc, xcb=xcb: e.activation(out=xcb, in_=xc, func=AF.Copy), reads=[B['xc']], writes=[B['xcb']])
                    bR, bRb = bank()
                    mm_group(bR, bRb, bR[:, :], [(lruw[:, 0, n, :], xcb)], [lruw_b, B['xcb']])
                    bI, bIb = bank()
                    mm_group(bI, bIb, bI[:, :], [(lruw[:, 1, n, :], xcb)], [lruw_b, B['xcb']])
                    P.op('act', lambda e, r_=r_, bR=bR, n=n: e.activation(out=r_, in_=bR[:, :], func=AF.Sigmoid, bias=par[:, PC['lru_ba'] + n:PC['lru_ba'] + n + 1]),
                         reads=[bRb, par_b], writes=[B['r']])
                    P.op('act', lambda e, i_=i_, bI=bI, n=n: e.activation(out=i_, in_=bI[:, :], func=AF.Sigmoid, bias=par[:, PC['lru_bx'] + n:PC['lru_bx'] + n + 1]),
                         reads=[bIb, par_b], writes=[B['i']])
                    P.op('act', lambda e, a_=a_, r_=r_, n=n: e.activation(out=a_, in_=r_, func=AF.Exp, scale=dpar[:, 24 + n:25 + n]), reads=[B['r'], dpar_b], writes=[B['a']])
                    P.op('act', lambda e, tt=tt, r_=r_, n=n: e.activation(out=tt, in_=r_, func=AF.Exp, scale=dpar[:, 32 + n:33 + n]), reads=[B['r'], dpar_b], writes=[B['t']])
                    P.op('dve', lambda e, tt=tt: e.tensor_scalar(out=tt, in0=tt, scalar1=-1.0, scalar2=1.0, op0=ALU.mult, op1=ALU.add), reads=[B['t']], writes=[B['t']])
                    P.op('act', lambda e, tt=tt: e.activation(out=tt, in_=tt, func=AF.Sqrt), reads=[B['t']], writes=[B['t']])
                    if tile == 0:
                        P.op('dve', lambda e, tt=tt: e.memset(tt[:, 0:1], 1.0), reads=[B['t']], writes=[B['t']])
                    P.op('dve', lambda e, u_=u_, xc=xc, i_=i_: e.tensor_tensor(out=u_, in0=xc, in1=i_, op=ALU.mult), reads=[B['xc'], B['i']], writes=[B['u']])
                    P.op('dve', lambda e, u_=u_, tt=tt: e.tensor_tensor(out=u_, in0=u_, in1=tt, op=ALU.mult), reads=[B['u'], B['t']], writes=[B['u']])
                    P.op('dve', lambda e, hh_=hh_, a_=a_, u_=u_, n=n: e.tensor_tensor_scan(out=hh_, data0=a_, data1=u_, initial=lru_st[:, n:n + 1], op0=ALU.mult, op1=ALU.add),
                         reads=[B['a'], B['u'], lrust_b[n]], writes=[B['hh']])
                    P.op('dve', lambda e, hh_=hh_, n=n: e.tensor_copy(out=lru_st[:, n:n + 1], in_=hh_[:, TT - 1:TT]), reads=[B['hh']], writes=[lrust_b[n]])
                    bG, bGb = bank()
                    proj4(wgb_, wgbb, jj, bG, bGb)
                    P.op('act', lambda e, xg=xg, bG=bG: e.activation(out=xg, in_=bG[:, :], func=AF.Copy), reads=[bGb], writes=[B['xg']])
                    P.op('dve', lambda e, tt=tt, xg=xg: e.tensor_tensor(out=tt, in0=xg, in1=xg, op=ALU.mult), reads=[B['xg'], B['t'], B['u']], writes=[B['t']])
                    P.op('dve', lambda e, tt=tt: e.tensor_scalar(out=tt, in0=tt, scalar1=0.044715, scalar2=1.0, op0=ALU.mult, op1=ALU.add), reads=[B['t']], writes=[B['t']])
                    P.op('dve', lambda e, tt=tt, xg=xg: e.tensor_tensor(out=tt, in0=tt, in1=xg, op=ALU.mult), reads=[B['t'], B['xg']], writes=[B['t']])
                    P.op('act', lambda e, tt=tt: e.activation(out=tt, in_=tt, func=AF.Sigmoid, scale=1.5957691216057308), reads=[B['t']], writes=[B['t']])
                    P.op('dve', lambda e, tt=tt, xg=xg: e.tensor_tensor(out=tt, in0=tt, in1=xg, op=ALU.mult), reads=[B['t'], B['xg']], writes=[B['t']])
                    P.op('dve', lambda e, tt=tt, hh_=hh_, n=n: e.tensor_tensor(out=yb[:, n, :], in0=tt, in1=hh_, op=ALU.mult), reads=[B['t'], B['hh']], writes=[yb_b[n]])
            P.barrier()
            arena.release(m1)
            if stop_after in ('yab', 'dbgh'):
                for kc in range(8):
                    P.op('dve', lambda e, kc=kc: e.tensor_copy(out=xT[:, kc, :], in_=ya[:, kc, :]), reads=[ya_b[kc], x_b[kc]], writes=[x_b[kc]])
                    P.op('dve', lambda e, kc=kc: e.tensor_copy(out=xT[:, 8 + kc, :], in_=yb[:, kc, :]), reads=[yb_b[kc], x_b[8 + kc]], writes=[x_b[8 + kc]])
                P.barrier()
                arena.release(m0)
                return
            m1 = arena.mark()
            mg = arena.alloc([16, TT], BF16, 'mg')
            mg_b = [Buf('mg%d' % i) for i in range(16)]
            s1 = [arena.alloc([TT], F32, 's1') for _ in range(2)]
            s2 = [arena.alloc([TT], F32, 's2') for _ in range(2)]
            s1b = [Buf('s1a'), Buf('s1b')]
            s2b = [Buf('s2a'), Buf('s2b')]
            for d in range(16):
                if True:
                    (wA, wB, wa_, wb_), wAb = next_w()
                    wBb = wab = wbb = wAb
                    q = d % 2
                    b1, b1b = bank(); proj4(wA, wAb, 0, b1, b1b)
                    b2, b2b = bank(); proj4(wB, wBb, 0, b2, b2b)
                    b3, b3b = bank()
                    mm_group(b3, b3b, b3[:, :], [(wa_[:, kc, :], ya[:, kc, :]) for kc in range(8)], [wab] + ya_b)
                    b4_, b4b = bank()
                    mm_group(b4_, b4b, b4_[:, :], [(wb_[:, kc, :], yb[:, kc, :]) for kc in range(8)], [wbb] + yb_b)
                    P.op('act', lambda e, q=q, b1=b1, d=d: e.activation(out=s1[q], in_=b1[:, :], func=AF.Sigmoid, bias=par[:, PC['bgA'] + d:PC['bgA'] + d + 1]),
                         reads=[b1b, par_b], writes=[s1b[q]])
                    P.op('act', lambda e, q=q, b2=b2, d=d: e.activation(out=s2[q], in_=b2[:, :], func=AF.Sigmoid, bias=par[:, PC['bgB'] + d:PC['bgB'] + d + 1]),
                         reads=[b2b, par_b], writes=[s2b[q]])
                    P.op('dve', lambda e, q=q, b3=b3: e.tensor_tensor(out=s1[q], in0=s1[q], in1=b3[:, :], op=ALU.mult), reads=[s1b[q], b3b], writes=[s1b[q]])
                    P.op('dve', lambda e, q=q, b4_=b4_: e.tensor_tensor(out=s2[q], in0=s2[q], in1=b4_[:, :], op=ALU.mult), reads=[s2b[q], b4b], writes=[s2b[q]])
                    P.op('dve', lambda e, q=q, d=d: e.tensor_tensor(out=mg[:, d, :], in0=s1[q], in1=s2[q], op=ALU.add), reads=[s1b[q], s2b[q]], writes=[mg_b[d]])
            proj_residual(mg, mg_b)
            P.barrier()
            arena.release(m1)
            arena.release(m0)

        def proj_residual(src, src_b):
            for dg in range(4):
                w_, wb_ = next_w()
                for dd in range(4):
                    d = dg * 4 + dd
                    bk, bkb = bank()
                    mm_group(bk, bkb, bk[:, :], [(w_[:, kc, dd * 128:(dd + 1) * 128], src[:, kc, :]) for kc in range(16)], [wb_] + src_b)
                    P.op('dve', lambda e, d=d, bk=bk: e.tensor_tensor(out=xT[:, d, :], in0=xT[:, d, :], in1=bk[:, :], op=ALU.add),
                         reads=[bkb, x_b[d]], writes=[x_b[d]])

        def xattn():
            m = arena.mark()
            qT = arena.alloc([16, TT], BF16, 'qT')
            oT = arena.alloc([16, TT], BF16, 'oT')
            pT = [arena.alloc([TT], BF16, 'pT%d' % i) for i in range(4)]
            rs = [arena.alloc([TT], F32, 'rs%d' % i) for i in range(2)]
            qT_b = [Buf('qT%d' % i) for i in range(16)]
            oT_b = [Buf('oT%d' % i) for i in range(16)]
            pT_b = [Buf('pT%d' % i) for i in range(4)]
            rs_b = [Buf('rs0'), Buf('rs1')]
            sc = float(512 ** -0.5)
            for dg in range(4):
                w_, wb_ = next_w()
                for dd in range(4):
                    d = dg * 4 + dd
                    bk, bkb = bank()
                    proj4(w_, wb_, dd, bk, bkb)
                    P.op('act', lambda e, d=d, bk=bk: e.activation(out=qT[:, d, :], in_=bk[:, :], func=AF.Copy, scale=sc), reads=[bkb], writes=[qT_b[d]])
            for h in range(4):
                pq = (h % 2) * 2
                for mc in range(2):
                    bk, bkb = bank()
                    mm_group(bk, bkb, bk[:, :], [(KT[:, 4 * h + kc, mc * 128:(mc + 1) * 128], qT[:, 4 * h + kc, :]) for kc in range(4)],
                             [KT_b] + qT_b[4 * h:4 * h + 4])
                    P.op('act', lambda e, mc=mc, pq=pq, bk=bk: e.activation(out=pT[pq + mc], in_=bk[:, :], func=AF.Exp), reads=[bkb], writes=[pT_b[pq + mc]])
                bs, bsb = bank()
                mm_group(bs, bsb, bs[:, :], [(ones[:, :], pT[pq + mc]) for mc in range(2)], [const_b, pT_b[pq], pT_b[pq + 1]])
                r_ = rs[h % 2]
                P.op('dve', lambda e, r_=r_, bs=bs: e.reciprocal(out=r_, in_=bs[:, :]), reads=[bsb], writes=[rs_b[h % 2]])
                for dc in range(4):
                    d = 4 * h + dc
                    bk, bkb = bank()
                    mm_group(bk, bkb, bk[:, :], [(Vt[:, mc, d * 128:(d + 1) * 128], pT[pq + mc]) for mc in range(2)], [V_b, pT_b[pq], pT_b[pq + 1]])
                    P.op('dve', lambda e, d=d, bk=bk, r_=r_: e.tensor_tensor(out=oT[:, d, :], in0=bk[:, :], in1=r_, op=ALU.mult), reads=[bkb, rs_b[h % 2]], writes=[oT_b[d]])
            proj_residual(oT, oT_b)
            P.barrier()
            arena.release(m)

        P.dma('sp', [lambda e: e.dma_start(out=par[:, :], in_=par_d)], P.dma_sem('m_par'), writes=[par_b])
        P.dma('sp', [lambda e: e.dma_start(out=maskT[:, :], in_=maskT_d), lambda e: e.dma_start(out=scanm[:, :], in_=scanm_d)], P.dma_sem('m_msk'), writes=[const_b])
        P.dma('pool', [lambda e: e.dma_start(out=ident[:, :], in_=ident_d)], P.dma_sem('m_id'), writes=[const_b])
        P.dma('pool', [lambda e: e.dma_start(out=lruw[:, 0, :, :], in_=W['lru_wa'].rearrange("n h k -> h n k")),
                       lambda e: e.dma_start(out=lruw[:, 1, :, :], in_=W['lru_wx'].rearrange("n h k -> h n k"))], P.dma_sem('m_lru'), writes=[lruw_b])
        P.op('dve', lambda e: e.memset(ones[:, :], 1.0), writes=[const_b], reads=[])
        P.op('dve', lambda e: e.memset(zero1[:, :], 0.0), writes=[const_b], reads=[])
        P.op('dve', lambda e: e.memset(state[:, :, :], 0.0), writes=state_b)
        P.op('dve', lambda e: e.memset(halo[:, :, :], 0.0), writes=halo_b)
        P.op('dve', lambda e: e.memset(lru_st[:, :], 0.0), writes=lrust_b)
        P.op('dve', lambda e: e.tensor_tensor(out=dpar[:, 40:48], in0=par[:, PC['lb0']:PC['lb0'] + 8], in1=par[:, PC['lb1']:PC['lb1'] + 8], op=ALU.subtract),
             reads=[par_b], writes=[dpar_b])
        P.op('act', lambda e: e.activation(out=dpar[:, 0:8], in_=dpar[:, 40:48], func=AF.Sigmoid), reads=[dpar_b], writes=[dpar_b])
        P.op('dve', lambda e: e.tensor_scalar(out=dpar[:, 8:16], in0=dpar[:, 0:8], scalar1=-1.0, scalar2=1.0, op0=ALU.mult, op1=ALU.add), reads=[dpar_b], writes=[dpar_b])
        P.op('dve', lambda e: e.tensor_scalar(out=dpar[:, 16:24], in0=dpar[:, 0:8], scalar1=-1.0, scalar2=None, op0=ALU.add), reads=[dpar_b], writes=[dpar_b])
        P.op('act', lambda e: e.activation(out=dpar[:, 40:48], in_=par[:, PC['lru_lambda']:PC['lru_lambda'] + 8], func=AF.Sigmoid), reads=[par_b, dpar_b], writes=[dpar_b])
        P.op('act', lambda e: e.activation(out=dpar[:, 40:48], in_=dpar[:, 40:48], func=AF.Ln), reads=[dpar_b], writes=[dpar_b])
        P.op('dve', lambda e: e.tensor_scalar(out=dpar[:, 24:32], in0=dpar[:, 40:48], scalar1=8.0, scalar2=None, op0=ALU.mult), reads=[dpar_b], writes=[dpar_b])
        P.op('dve', lambda e: e.tensor_scalar(out=dpar[:, 32:40], in0=dpar[:, 40:48], scalar1=16.0, scalar2=None, op0=ALU.mult), reads=[dpar_b], writes=[dpar_b])

        m = arena.mark()
        mT = arena.alloc([16, NM], F32, 'mT')
        mTn = arena.alloc([16, NM], BF16, 'mTn')
        msq = arena.alloc([16, NM], BF16, 'msq')
        mrs = arena.alloc([NM], F32, 'mrs')
        mT_b, mTn_b, msq_b, mrs_b = Buf('mT'), Buf('mTn'), Buf('msq'), Buf('mrs')
        P.dma('sp', [lambda e: e.dma_start(out=mT, in_=kview(memT_d))], P.dma_sem('m_mem'), writes=[mT_b])
        P.op('act', lambda e: e.activation(out=msq, in_=mT, func=AF.Square), reads=[mT_b], writes=[msq_b])
        bk, bkb = bank()
        mm_group(bk, bkb, bk[:, 0:NM], [(ones[:, :], msq[:, kc, :]) for kc in range(16)], [msq_b, const_b])
        rstd_from(bk[:, 0:NM], bkb, mrs, mrs_b, 1.0 / D)
        for kc in range(16):
            gc = PC['mem_norm'] + kc
            P.op('dve', lambda e, kc=kc, gc=gc: e.scalar_tensor_tensor(out=mTn[:, kc, :], in0=mT[:, kc, :], scalar=par[:, gc:gc + 1], in1=mrs, op0=ALU.mult, op1=ALU.mult),
                 reads=[mT_b, mrs_b, par_b], writes=[mTn_b])
        for g in range(4):
            w_, wb_ = next_w()
            for dd in range(4):
                d = g * 4 + dd
                bk, bkb = bank()
                mm_group(bk, bkb, bk[:, 0:NM], [(w_[:, kc, dd * 128:(dd + 1) * 128], mTn[:, kc, :]) for kc in range(16)], [wb_, mTn_b])
                P.op('act', lambda e, d=d, bk=bk: e.activation(out=KT[:, d, :], in_=bk[:, 0:NM], func=AF.Copy), reads=[bkb], writes=[KT_b])
        for g in range(4):
            w_, wb_ = next_w()
            for mc in range(2):
                bk, bkb = bank()
                mm_group(bk, bkb, bk[:, :], [(mTn[:, kc, mc * 128:(mc + 1) * 128], w_[:, kc, :]) for kc in range(16)], [wb_, mTn_b])
                P.op('act', lambda e, g=g, mc=mc, bk=bk: e.activation(out=Vt[:, mc, g * 512:(g + 1) * 512], in_=bk[:, :], func=AF.Copy), reads=[bkb], writes=[V_b])
        P.barrier()
        arena.release(m)

        xTv = kview(xT_d)
        oTv = kview(outT_d)
        last_out = None
        for tile in range(ntiles):
            ts_ = slice(tile * TT, (tile + 1) * TT)
            P.dma('sp', [lambda e, g=g, ts_=ts_: e.dma_start(out=xT[:, 4 * g:4 * g + 4, :], in_=xTv[:, 4 * g:4 * g + 4, ts_]) for g in range(4)], x_sem, writes=x_b)
            stages = [('ffn1', lambda: (rmsnorm_to_hT(PC['ffn1_norm']), ffn())),
                      ('mix', lambda: (rmsnorm_to_hT(PC['mix_norm']), mixer(tile))),
                      ('xattn', lambda: (rmsnorm_to_hT(PC['xattn_norm']), xattn())),
                      ('ffn2', lambda: (rmsnorm_to_hT(PC['ffn2_norm']), ffn()))]
            stopped = False
            for sname, sfn in stages:
                sfn()
                if stop_after == sname or (stop_after in ('yab', 'dbgh') and sname == 'mix'):
                    stopped = True
                    break
            m = arena.mark()
            of = arena.alloc([16, TT], F32, 'of')
            of_b = Buf('of')
            if stopped:
                for kc in range(16):
                    P.op('dve', lambda e, kc=kc: e.tensor_copy(out=of[:, kc, :], in_=xT[:, kc, :]), reads=[x_b[kc]], writes=[of_b])
                assert ntiles == 1, "stop_after only supported with ntiles == 1"
                del items[max(wptr['issued'], wptr['next']):]
                wptr['next'] = len(items)
            else:
                sq = arena.alloc([16, TT], BF16, 'fsq')
                rstd = arena.alloc([TT], F32, 'frstd')
                sq_b, rstd_b = Buf('fsq'), Buf('frstd')
                for kc in range(16):
                    P.op('act', lambda e, kc=kc: e.activation(out=sq[:, kc, :], in_=xT[:, kc, :], func=AF.Square), reads=[x_b[kc]], writes=[sq_b])
                bk, bkb = bank()
                mm_group(bk, bkb, bk[:, :], [(ones[:, :], sq[:, kc, :]) for kc in range(16)], [sq_b, const_b])
                rstd_from(bk[:, :], bkb, rstd, rstd_b, 1.0 / D)
                for kc in range(16):
                    gc = PC['final_norm'] + kc
                    P.op('dve', lambda e, kc=kc, gc=gc: e.scalar_tensor_tensor(out=of[:, kc, :], in0=xT[:, kc, :], scalar=par[:, gc:gc + 1], in1=rstd, op0=ALU.mult, op1=ALU.mult),
                         reads=[x_b[kc], rstd_b, par_b], writes=[of_b])
            last_out = P.dma('sp', [lambda e, g=g, ts_=ts_: e.dma_start(out=oTv[:, 4 * g:4 * g + 4, ts_], in_=of[:, 4 * g:4 * g + 4, :]) for g in range(4)], o_sem, reads=[of_b])
            for e_ in ('act', 'dve'):
                P.wait_tok(e_, last_out)
            P.barrier()
            arena.release(m)
        P.wait_tok('sp', last_out)
        for tk_ in dbg_list:
            P.wait_tok('sp', tk_)
        for ds_ in slot_sem:
            if ds_.cnt > 0:
                P.wait_tok('sp', (ds_.key, ds_.cnt))
        assert wptr['next'] == len(items), (wptr, len(items))
        P.emit()
        print("program ops:", P.nops, {e: len(P.ops[e]) for e in ENGS})
    return nc


def make_in_maps(inputs, ncores=8):
    par = pack_params(inputs)
    ident, maskT, sm = make_consts()
    shared = {
        'par': par, 'ident': ident, 'maskT': maskT, 'scanm': sm,
        'ffn1_w_up': np.ascontiguousarray(inputs['ffn1_w_up'][0]), 'ffn1_w_down': np.ascontiguousarray(inputs['ffn1_w_down'][0]),
        'w_in': np.ascontiguousarray(inputs['w_in'][0]),
        'lru_wa': np.ascontiguousarray(inputs['lru_wa'][0]), 'lru_wx': np.ascontiguousarray(inputs['lru_wx'][0]),
        'w_branch_a': np.ascontiguousarray(inputs['w_branch_a'][0]), 'w_branch_b': np.ascontiguousarray(inputs['w_branch_b'][0]),
        'w_out': np.ascontiguousarray(inputs['w_out'][0]), 'xattn_wq': np.ascontiguousarray(inputs['xattn_wq'][0]),
        'xattn_wkv': np.ascontiguousarray(inputs['xattn_wkv'][0]), 'xattn_wo': np.ascontiguousarray(inputs['xattn_wo'][0]),
        'ffn2_w_up': np.ascontiguousarray(inputs['ffn2_w_up'][0]), 'ffn2_w_down': np.ascontiguousarray(inputs['ffn2_w_down'][0]),
    }
    shared = {k: np.asarray(v, dtype=np.float32) for k, v in shared.items()}
    maps = []
    for b in range(ncores):
        m = dict(shared)
        m['xT'] = np.ascontiguousarray(np.asarray(inputs['x'][b], dtype=np.float32).T)
        m['memT'] = np.ascontiguousarray(np.asarray(inputs['mem'][b], dtype=np.float32).T)
        maps.append(m)
    return maps


def kernel(**inputs):
    inputs = {k: np.asarray(v) for k, v in inputs.items()}
    nc = build_program(ntiles=8)
    maps = make_in_maps(inputs, 8)
    res = run_bass_kernel_spmd(nc, maps, core_ids=list(range(8)))
    out = np.stack([np.ascontiguousarray(res.results[b]["outT"].T) for b in range(8)], axis=0)
    return out.astype(np.float32)
```
